# Optimizing a Trainium2 kernel written in Bass

```python
import math
import jax, jax.numpy as jnp
from jax import lax
import numpy as np

D_MODEL = 1024
BATCH = 4
SEQ = 4096
DEPTH = 4

CHUNK = 64
N_MIXERS = 3
EPS = 1e-6
N_FOX = (DEPTH + 2) // 3
N_S5 = (DEPTH + 1) // 3
N_POOL = DEPTH // 3
FOX_HEADS = 16
FOX_HEAD_DIM = D_MODEL // FOX_HEADS
Q_BLOCK = 128
FOX_IN = 3 * D_MODEL + FOX_HEADS
FORGET_BIAS_CENTER = 3.0
S5_GROUP = 16
S5_GROUPS = D_MODEL // S5_GROUP
S5_STATE = 64
S5_DT_MIN = 1e-3
S5_DT_MAX = 1e-1
POOL_WINDOWS = (2, 4, 8, 16)
POOL_GROUPS = len(POOL_WINDOWS)
POOL_WIDTH = D_MODEL // POOL_GROUPS
D_FF = -(-8 * D_MODEL // (3 * 256)) * 256

kernel_name = "interleaved_fox_s5_pool_hybrid"

F32 = jnp.float32


def rms_norm(x, g):
    xf = x.astype(F32)
    y = xf * lax.rsqrt(jnp.mean(xf * xf, axis=-1, keepdims=True) + EPS)
    return (y * g.astype(F32)).astype(x.dtype)


def forgetting_attention(h, w_in, b_f, w_out):
    B, S, _ = h.shape
    proj = h @ w_in
    q, k, v, f_logit = jnp.split(proj, [D_MODEL, 2 * D_MODEL, 3 * D_MODEL], axis=-1)

    def heads(t):
        return t.reshape(B, S, FOX_HEADS, FOX_HEAD_DIM).transpose(0, 2, 1, 3).astype(F32)

    q, k, v = heads(q), heads(k), heads(v)
    log_f = jax.nn.log_sigmoid(f_logit.astype(F32) + b_f.astype(F32))
    c = jnp.cumsum(log_f, axis=1).transpose(0, 2, 1)
    n_blk = S // Q_BLOCK
    q_blocks = q.reshape(B, FOX_HEADS, n_blk, Q_BLOCK, FOX_HEAD_DIM).transpose(2, 0, 1, 3, 4)
    c_blocks = c.reshape(B, FOX_HEADS, n_blk, Q_BLOCK).transpose(2, 0, 1, 3)
    starts = jnp.arange(n_blk, dtype=jnp.int32) * Q_BLOCK
    key_pos = jnp.arange(S, dtype=jnp.int32)
    scale = FOX_HEAD_DIM ** -0.5

    def one_block(args):
        q_b, c_b, start = args
        logits = jnp.einsum('bhqd,bhkd->bhqk', q_b, k) * scale
        logits = logits + c_b[..., :, None] - c[..., None, :]
        q_pos = start + jnp.arange(Q_BLOCK, dtype=jnp.int32)
        mask = key_pos[None, :] <= q_pos[:, None]
        p = jax.nn.softmax(jnp.where(mask, logits, -jnp.inf), axis=-1)
        return jnp.einsum('bhqk,bhkd->bhqd', p, v)

    out = lax.map(one_block, (q_blocks, c_blocks, starts))
    out = out.transpose(1, 0, 3, 2, 4).reshape(B, S, D_MODEL).astype(h.dtype)
    return out @ w_out


def s5_layer(h, w_in, a_re, a_im, log_dt, b_re, b_im, c_re, c_im, d, w_glu):
    B, S, _ = h.shape
    u = (h @ w_in).astype(F32)
    u_g = u.reshape(B, S, S5_GROUPS, S5_GROUP).astype(jnp.complex64)
    lam = lax.complex(a_re.astype(F32), a_im.astype(F32))
    dt = jnp.exp(log_dt.astype(F32))[:, None]
    lam_bar = jnp.exp(lam * dt)
    b_mat = lax.complex(b_re.astype(F32), b_im.astype(F32))
    b_bar = ((lam_bar - 1.0) / lam)[..., None] * b_mat
    bu = jnp.einsum('gpc,bsgc->bsgp', b_bar, u_g)
    lam_seq = jnp.broadcast_to(lam_bar, bu.shape)

    def combine(left, right):
        a_l, s_l = left
        a_r, s_r = right
        return a_r * a_l, a_r * s_l + s_r

    _, states = lax.associative_scan(combine, (lam_seq, bu), axis=1)
    c_mat = lax.complex(c_re.astype(F32), c_im.astype(F32))
    y = jnp.einsum('gcp,bsgp->bsgc', c_mat, states).real.reshape(B, S, D_MODEL)
    y = y + d.astype(F32) * u
    g = jax.nn.gelu(y).astype(h.dtype)
    val, gate = jnp.split(g @ w_glu, 2, axis=-1)
    return val * jax.nn.sigmoid(gate)


def multiscale_pool(h, w_grp, b_grp, scale):
    B, S, _ = h.shape
    hf = h.astype(F32).reshape(B, S, POOL_GROUPS, POOL_WIDTH)
    count_base = jnp.arange(1, S + 1, dtype=F32)[None, :, None]
    outs = []
    for gi, w in enumerate(POOL_WINDOWS):
        xg = hf[:, :, gi]
        cs = jnp.cumsum(xg, axis=1)
        lag = jnp.pad(cs[:, :-w], ((0, 0), (w, 0), (0, 0)))
        mean = (cs - lag) / jnp.minimum(count_base, float(w))
        outs.append(jnp.einsum('bsc,cd->bsd', mean - xg, w_grp[gi].astype(F32)))
    y = jnp.concatenate(outs, axis=-1) + b_grp.astype(F32)
    return (y * scale.astype(F32)).astype(h.dtype)


def swiglu(h, w_gate_up, w_down):
    g, u = jnp.split(h @ w_gate_up, 2, axis=-1)
    return (jax.nn.silu(g) * u) @ w_down


def setup_inputs(seed: int = 0) -> dict:
    key = jax.random.key(seed)
    ks = jax.random.split(key, 24)
    D, H, G, P, C = D_MODEL, FOX_HEADS, S5_GROUPS, S5_STATE, S5_GROUP
    nrm = jax.random.normal
    x = nrm(ks[0], (BATCH, SEQ, D), F32)
    mix_norm_g = 1.0 + 0.05 * nrm(ks[1], (DEPTH, D), F32)
    ffn_norm_g = 1.0 + 0.05 * nrm(ks[2], (DEPTH, D), F32)
    final_norm_g = 1.0 + 0.05 * nrm(ks[3], (D,), F32)
    fox_w_in = nrm(ks[4], (N_FOX, D, FOX_IN), F32) * D ** -0.5
    fox_b_f = FORGET_BIAS_CENTER + 0.5 * nrm(ks[5], (N_FOX, H), F32)
    fox_w_out = nrm(ks[6], (N_FOX, D, D), F32) * D ** -0.5
    s5_w_in = nrm(ks[7], (N_S5, D, D), F32) * D ** -0.5
    s5_a_re = -0.5 + 0.01 * nrm(ks[8], (N_S5, G, P), F32)
    s5_a_im = math.pi * jnp.arange(P, dtype=F32)[None, None, :] + 0.01 * nrm(ks[9], (N_S5, G, P), F32)
    s5_log_dt = jax.random.uniform(ks[10], (N_S5, G), F32, math.log(S5_DT_MIN), math.log(S5_DT_MAX))
    s5_b_re = nrm(ks[11], (N_S5, G, P, C), F32) * (2 * C) ** -0.5
    s5_b_im = nrm(ks[12], (N_S5, G, P, C), F32) * (2 * C) ** -0.5
    s5_c_re = nrm(ks[13], (N_S5, G, C, P), F32) * P ** -0.5
    s5_c_im = nrm(ks[14], (N_S5, G, C, P), F32) * P ** -0.5
    s5_d = nrm(ks[15], (N_S5, D), F32)
    s5_w_glu = nrm(ks[16], (N_S5, D, 2 * D), F32) * D ** -0.5
    pool_w = nrm(ks[17], (N_POOL, POOL_GROUPS, POOL_WIDTH, POOL_WIDTH), F32) * POOL_WIDTH ** -0.5
    pool_b = 0.01 * nrm(ks[18], (N_POOL, D), F32)
    pool_scale = 0.5 + 0.05 * nrm(ks[19], (N_POOL, D), F32)
    ffn_w_gate_up = nrm(ks[20], (DEPTH, D, 2 * D_FF), F32) * D ** -0.5
    ffn_w_down = nrm(ks[21], (DEPTH, D_FF, D), F32) * D_FF ** -0.5
    return {"x": x, "mix_norm_g": mix_norm_g, "ffn_norm_g": ffn_norm_g, "final_norm_g": final_norm_g,
            "fox_w_in": fox_w_in, "fox_b_f": fox_b_f, "fox_w_out": fox_w_out,
            "s5_w_in": s5_w_in, "s5_a_re": s5_a_re, "s5_a_im": s5_a_im, "s5_log_dt": s5_log_dt,
            "s5_b_re": s5_b_re, "s5_b_im": s5_b_im, "s5_c_re": s5_c_re, "s5_c_im": s5_c_im,
            "s5_d": s5_d, "s5_w_glu": s5_w_glu,
            "pool_w": pool_w, "pool_b": pool_b, "pool_scale": pool_scale,
            "ffn_w_gate_up": ffn_w_gate_up, "ffn_w_down": ffn_w_down}


def reference(x, mix_norm_g, ffn_norm_g, final_norm_g, fox_w_in, fox_b_f, fox_w_out,
              s5_w_in, s5_a_re, s5_a_im, s5_log_dt, s5_b_re, s5_b_im, s5_c_re, s5_c_im,
              s5_d, s5_w_glu, pool_w, pool_b, pool_scale, ffn_w_gate_up, ffn_w_down):
    for i in range(DEPTH):
        h = rms_norm(x, mix_norm_g[i])
        kind, j = i % N_MIXERS, i // N_MIXERS
        if kind == 0:
            mix = forgetting_attention(h, fox_w_in[j], fox_b_f[j], fox_w_out[j])
        elif kind == 1:
            mix = s5_layer(h, s5_w_in[j], s5_a_re[j], s5_a_im[j], s5_log_dt[j], s5_b_re[j], s5_b_im[j],
                           s5_c_re[j], s5_c_im[j], s5_d[j], s5_w_glu[j])
        else:
            mix = multiscale_pool(h, pool_w[j], pool_b[j], pool_scale[j])
        x = x + mix.astype(x.dtype)
        x = x + swiglu(rms_norm(x, ffn_norm_g[i]), ffn_w_gate_up[i], ffn_w_down[i]).astype(x.dtype)
    return rms_norm(x, final_norm_g)
```

```python
import numpy as np
import ml_dtypes
from contextlib import ExitStack
import concourse.bass as bass
import concourse.mybir as mybir
from concourse.bass_utils import run_bass_kernel_spmd

F32 = mybir.dt.float32
BF16 = mybir.dt.bfloat16
AF = mybir.ActivationFunctionType
ALU = mybir.AluOpType
NPBF = ml_dtypes.bfloat16

D = 1024
S = 4096
B = 4
H = 16
DFF = 2816
EPS = 1e-6
NCORES = 8


class Buf:
    __slots__ = ("name", "writers", "readers", "sem", "cnt")

    def __init__(self, name):
        self.name = name
        self.writers = []
        self.readers = []
        self.sem = None
        self.cnt = 0


class Op:
    __slots__ = ("eng", "fn", "deps", "needs_inc", "val", "sem", "is_dma", "waits")


class Prog:
    ENGS = ["pe", "act", "dve", "pool", "sp"]

    def __init__(self, nc, es):
        self.nc = nc
        self.es = es
        self.ops = {e: [] for e in self.ENGS}
        self.esem = {e: es.enter_context(nc.semaphore("sem_" + e)) for e in ["pe", "act", "dve", "pool"]}
        self.bufs = []
        self.nb = 0

    def buf(self, name=None):
        self.nb += 1
        b = Buf(name or f"b{self.nb}")
        self.bufs.append(b)
        return b

    def sb(self, name, shape, dtype):
        return self.es.enter_context(self.nc.sbuf_tensor("sb_" + name, list(shape), dtype))

    def ps(self, name, shape, dtype=F32):
        return self.es.enter_context(self.nc.psum_tensor("ps_" + name, list(shape), dtype))

    def op(self, eng, fn, reads=(), writes=(), acc=False, dma=None):
        o = Op()
        o.eng = eng
        o.fn = fn
        o.deps = []
        o.needs_inc = False
        o.val = 0
        o.sem = None
        o.is_dma = dma is not None
        o.waits = None
        for b in reads:
            for w in b.writers:
                o.deps.append((w, True))
        for b in writes:
            for r in b.readers:
                if r is not o:
                    o.deps.append((r, False))
            if not acc:
                for w in b.writers:
                    o.deps.append((w, False))
        for b in reads:
            b.readers.append(o)
        for b in writes:
            if acc:
                b.writers.append(o)
            else:
                b.writers = [o]
                b.readers = [r for r in b.readers if r is o]
        if dma is not None:
            if dma.sem is None:
                dma.sem = self.es.enter_context(self.nc.semaphore("dsem_" + dma.name))
            dma.cnt += 16
            o.sem = dma.sem
            o.val = dma.cnt
        self.ops[eng].append(o)
        return o

    def dma(self, eng, out_ap, in_ap, b, reads=(), writes=(), acc=False):
        return self.op(eng, lambda e: e.dma_start(out=out_ap, in_=in_ap), reads=reads, writes=writes,
                       acc=acc, dma=b)

    def finalize(self):
        for eng in self.ENGS:
            for o in self.ops[eng]:
                need = []
                for (p, raw) in o.deps:
                    if p.is_dma:
                        need.append(p)
                        continue
                    if p.eng == o.eng and not o.is_dma:
                        if o.eng == "pe" or not raw:
                            continue
                    p.needs_inc = True
                    need.append(p)
                o.deps = need
        for eng in ["pe", "act", "dve", "pool"]:
            c = 0
            for o in self.ops[eng]:
                if o.is_dma:
                    continue
                if o.needs_inc:
                    c += 1
                    o.val = c
                    o.sem = self.esem[eng]
        for eng in self.ENGS:
            for o in self.ops[eng]:
                w = {}
                for p in o.deps:
                    k = id(p.sem)
                    if k not in w or w[k][1] < p.val:
                        w[k] = (p.sem, p.val)
                o.waits = list(w.values())

    def emit(self):
        self.finalize()
        nc = self.nc
        prog = self

        def run(engname, e):
            seen = {}
            for o in prog.ops[engname]:
                for (sem, val) in o.waits:
                    k = id(sem)
                    if seen.get(k, 0) < val:
                        e.wait_ge(sem, val)
                        seen[k] = val
                inst = o.fn(e)
                if o.is_dma:
                    inst.then_inc(o.sem, 16)
                elif o.needs_inc:
                    inst.then_inc(o.sem, 1)
            if engname == "sp":
                for b in prog.bufs:
                    if b.sem is not None and seen.get(id(b.sem), 0) < b.cnt:
                        e.wait_ge(b.sem, b.cnt)

        with nc.Block() as block:
            @block.tensor
            def _(e):
                run("pe", e)

            @block.scalar
            def _(e):
                run("act", e)

            @block.vector
            def _(e):
                run("dve", e)

            @block.gpsimd
            def _(e):
                run("pool", e)

            @block.sync
            def _(e):
                run("sp", e)


class Ctx:
    def __init__(self, P, nt):
        self.P = P
        self.nt = nt
        nc = P.nc
        self.ones = P.sb("ones_bf", [128, 128], BF16)
        self.b_ones = P.buf("ones")
        P.op("pool", lambda e: e.memset(self.ones[:], 1.0), writes=[self.b_ones])
        self.sq = P.sb("sq", [128, 8, nt], BF16)
        self.b_sq = P.buf("sq")
        self.ss_ps = P.ps("ss_ps", [128, 512])
        self.b_ss = P.buf("ss_ps")
        self.rstd = P.sb("rstd", [128, nt], F32)
        self.b_rstd = P.buf("rstd")
        self.epsb = P.sb("epsb", [128, 1], F32)
        self.b_eps = P.buf("epsb")
        P.op("pool", lambda e: e.memset(self.epsb[:], EPS), writes=[self.b_eps])


def emit_norm(P, C, x_ap, b_x, g_ap, b_g, out_ap, b_out, n):
    P.op("act", lambda e: e.activation(out=C.sq[:, :, 0:n], in_=x_ap, func=AF.Square),
         reads=[b_x], writes=[C.b_sq])
    for kc in range(8):
        P.op("pe", lambda e, kc=kc: e.matmul(C.ss_ps[:, 0:n], lhsT=C.ones[:], rhs=C.sq[:, kc, 0:n],
                                              start=(kc == 0), stop=(kc == 7)),
             reads=[C.b_sq, C.b_ones], writes=[C.b_ss], acc=(kc > 0))
    P.op("act", lambda e: e.activation(out=C.rstd[:, 0:n], in_=C.ss_ps[:, 0:n], func=AF.Sqrt,
                                       bias=C.epsb[:, 0:1], scale=1.0 / D),
         reads=[C.b_ss, C.b_eps], writes=[C.b_rstd])
    P.op("dve", lambda e: e.reciprocal(out=C.rstd[:, 0:n], in_=C.rstd[:, 0:n]),
         reads=[C.b_rstd], writes=[C.b_rstd])
    for kc in range(8):
        P.op("dve", lambda e, kc=kc: e.scalar_tensor_tensor(
            out=out_ap[:, kc, :], in0=x_ap[:, kc, :], scalar=g_ap[:, kc:kc + 1], in1=C.rstd[:, 0:n],
            op0=ALU.mult, op1=ALU.mult),
             reads=[b_x, b_g, C.b_rstd], writes=[b_out], acc=(kc > 0))


def load_weight_bf16(P, name, dram_ap, kchunks, ncols, nsplit=4, eng="pool"):
    t = P.sb(name, [128, kchunks, ncols], BF16)
    b = P.buf(name)
    src = dram_ap.rearrange("(kc p) n -> p kc n", p=128)
    step = (ncols + nsplit - 1) // nsplit
    first = True
    c0 = 0
    while c0 < ncols:
        c1 = min(ncols, c0 + step)
        P.dma(eng, t[:, :, c0:c1], src[:, :, c0:c1], b, writes=[b], acc=not first)
        first = False
        c0 = c1
    return t, b


def load_small(P, name, dram_ap, shape, dtype=F32, eng="sp"):
    t = P.sb(name, shape, dtype)
    b = P.buf(name)
    P.dma(eng, t[:], dram_ap, b, writes=[b])
    return t, b


class FFN:
    def __init__(self, P, C, wgu_d, wd_d, gffn_d):
        self.wgu, self.b_wgu = load_weight_bf16(P, "wgu", wgu_d, 8, 2 * DFF, nsplit=8)
        self.wd, self.b_wd = load_weight_bf16(P, "wd", wd_d, 22, D, nsplit=4)
        self.g, self.b_g = load_small(P, "gffn", gffn_d, [128, 8])
        nt = C.nt
        self.h = P.sb("ffn_h", [128, 8, nt], BF16)
        self.b_h = P.buf("ffn_h")
        self.act = P.sb("ffn_act", [128, 22, nt], BF16)
        self.b_act = [P.buf(f"ffn_act{i}") for i in range(22)]
        self.gps = [P.ps(f"g_ps{i}", [128, 512]) for i in range(2)]
        self.ups = [P.ps(f"u_ps{i}", [128, 512]) for i in range(2)]
        self.b_gps = [P.buf(f"g_ps{i}") for i in range(2)]
        self.b_ups = [P.buf(f"u_ps{i}") for i in range(2)]
        self.dps = [P.ps(f"d_ps{i}", [128, 512]) for i in range(2)]
        self.b_dps = [P.buf(f"d_ps{i}") for i in range(2)]
        self.sg = [P.sb(f"ffn_sg{i}", [128, nt], F32) for i in range(2)]
        self.b_sg = [P.buf(f"ffn_sg{i}") for i in range(2)]
        self.cnt = 0


def emit_ffn(P, C, F, x_ap, b_x, n):
    emit_norm(P, C, x_ap, b_x, F.g, F.b_g, F.h[:, :, 0:n], F.b_h, n)
    for i in range(22):
        s = F.cnt % 2
        F.cnt += 1
        for kc in range(8):
            P.op("pe", lambda e, kc=kc, i=i, s=s: e.matmul(
                F.gps[s][:, 0:n], lhsT=F.wgu[:, kc, i * 128:(i + 1) * 128], rhs=F.h[:, kc, 0:n],
                start=(kc == 0), stop=(kc == 7)),
                 reads=[F.b_wgu, F.b_h], writes=[F.b_gps[s]], acc=(kc > 0))
        for kc in range(8):
            P.op("pe", lambda e, kc=kc, i=i, s=s: e.matmul(
                F.ups[s][:, 0:n], lhsT=F.wgu[:, kc, DFF + i * 128:DFF + (i + 1) * 128], rhs=F.h[:, kc, 0:n],
                start=(kc == 0), stop=(kc == 7)),
                 reads=[F.b_wgu, F.b_h], writes=[F.b_ups[s]], acc=(kc > 0))
        P.op("act", lambda e, s=s: e.activation(out=F.sg[s][:, 0:n], in_=F.gps[s][:, 0:n], func=AF.Silu),
             reads=[F.b_gps[s]], writes=[F.b_sg[s]])
        P.op("dve", lambda e, s=s, i=i: e.tensor_tensor(out=F.act[:, i, 0:n], in0=F.ups[s][:, 0:n],
                                                       in1=F.sg[s][:, 0:n], op=ALU.mult),
             reads=[F.b_ups[s], F.b_sg[s]], writes=[F.b_act[i]])
    for j in range(8):
        s = j % 2
        for i in range(22):
            P.op("pe", lambda e, i=i, j=j, s=s: e.matmul(
                F.dps[s][:, 0:n], lhsT=F.wd[:, i, j * 128:(j + 1) * 128], rhs=F.act[:, i, 0:n],
                start=(i == 0), stop=(i == 21)),
                 reads=[F.b_wd, F.b_act[i]], writes=[F.b_dps[s]], acc=(i > 0))
        P.op("dve", lambda e, j=j, s=s: e.tensor_tensor(out=x_ap[:, j, :], in0=F.dps[s][:, 0:n],
                                                       in1=x_ap[:, j, :], op=ALU.add),
             reads=[F.b_dps[s], b_x], writes=[b_x])


def build_B(mode, T, final):
    nc = bass.Bass("TRN2", target_bir_lowering=False)
    NT = 256
    NB = 1 if mode == "glu" else 2
    ntile = T // NT
    dt = lambda name, shape, dtype=F32, kind="ExternalInput": nc.dram_tensor(name, list(shape), dtype, kind=kind).ap()
    xT_d = dt("xT", [D, T])
    wgu_d = dt("wgu", [D, 2 * DFF])
    wd_d = dt("wd", [DFF, D])
    gffn_d = dt("gffn", [128, 8])
    if mode in ("proj", "glu"):
        aT_d = dt("aT", [D, T], BF16)
        wm_d = dt("wm", [D, D if mode == "proj" else 2 * D])
    if mode == "pool":
        halo_d = dt("halo", [D, 16])
        gmix_d = dt("gmix", [128, 8])
        pw_d = dt("pw", [4, 256, 256])
        pb_d = dt("pb", [128, 8])
        psc_d = dt("psc", [128, 8])
        inv_d = dt("inv16", [128, 4 * 16])
    if final:
        gfin_d = dt("gfin", [128, 8])
    out_d = dt("outT", [D, T], F32, kind="ExternalOutput")

    with ExitStack() as es:
        P = Prog(nc, es)
        C = Ctx(P, NT)
        F = FFN(P, C, wgu_d, wd_d, gffn_d)
        if mode in ("proj", "glu"):
            ncol = D if mode == "proj" else 2 * D
            wm, b_wm = load_weight_bf16(P, "wm", wm_d, 8, ncol, nsplit=2)
            a_sb = [P.sb(f"a_sb{i}", [128, 8, NT], BF16) for i in range(NB)]
            b_a = [P.buf(f"a_sb{i}") for i in range(NB)]
            mps = F.dps
            b_mps = F.b_dps
            if mode == "glu":
                sgm = P.sb("sgm", [128, NT], F32)
                b_sgm = P.buf("sgm")
                tmpm = P.sb("tmpm", [128, NT], F32)
                b_tmpm = P.buf("tmpm")
        if mode == "pool":
            gmix, b_gmix = load_small(P, "gmix", gmix_d, [128, 8])
            pb, b_pb = load_small(P, "pb", pb_d, [128, 8])
            psc, b_psc = load_small(P, "psc", psc_d, [128, 8])
            inv16, b_inv = load_small(P, "inv16", inv_d, [128, 64])
            pw = P.sb("pw", [128, 4, 2, 256], BF16)
            b_pw = P.buf("pw")
            P.dma("pool", pw[:], pw_d.rearrange("g (kk p) n -> p g kk n", p=128), b_pw, writes=[b_pw])
            halo = P.sb("halo", [128, 8, 16], F32)
            b_halo = P.buf("halo")
            P.dma("sp", halo[:], halo_d.rearrange("(kc p) n -> p kc n", p=128), b_halo, writes=[b_halo])
            hb = P.sb("hb", [128, 8, 16 + NT], F32)
            b_hb = P.buf("hb")
            sA = P.sb("sA", [128, 16 + NT], F32)
            sB = P.sb("sB", [128, 16 + NT], F32)
            b_sA = P.buf("sA")
            b_sB = P.buf("sB")
            diff = P.sb("diff", [128, 8, NT], BF16)
            b_diff = P.buf("diff")
            mps = F.dps
            b_mps = F.b_dps
            tmpm = P.sb("tmpm", [128, NT], F32)
            b_tmpm = P.buf("tmpm")
            fix = P.sb("fix", [128, 16], F32)
            b_fix = P.buf("fix")
        if final:
            gfin, b_gfin = load_small(P, "gfin", gfin_d, [128, 8])
            fo = P.sb("fo", [128, 8, NT], F32)
            b_fo = P.buf("fo")
        x_sb = [P.sb(f"x_sb{i}", [128, 8, NT], F32) for i in range(NB)]
        b_x = [P.buf(f"x_sb{i}") for i in range(NB)]
        xT_v = xT_d.rearrange("(kc p) t -> p kc t", p=128)
        out_v = out_d.rearrange("(kc p) t -> p kc t", p=128)
        if mode in ("proj", "glu"):
            aT_v = aT_d.rearrange("(kc p) t -> p kc t", p=128)

        if mode == "pool":
            emit_norm(P, C, halo[:], b_halo, gmix, b_gmix, hb[:, :, 0:16], b_hb, 16)

        for tt in range(ntile):
            s = tt % NB
            xs = x_sb[s]
            bx = b_x[s]
            t0 = tt * NT
            P.dma("sp", xs[:], xT_v[:, :, t0:t0 + NT], bx, writes=[bx])
            if mode in ("proj", "glu"):
                P.dma("sp", a_sb[s][:], aT_v[:, :, t0:t0 + NT], b_a[s], writes=[b_a[s]])
                for j in range(8):
                    if mode == "proj":
                        m = mps[j % 2]
                        bm = b_mps[j % 2]
                        for kc in range(8):
                            P.op("pe", lambda e, kc=kc, j=j, m=m, s=s: e.matmul(
                                m[:, 0:NT], lhsT=wm[:, kc, j * 128:(j + 1) * 128], rhs=a_sb[s][:, kc, :],
                                start=(kc == 0), stop=(kc == 7)),
                                 reads=[b_wm, b_a[s]], writes=[bm], acc=(kc > 0))
                        P.op("dve", lambda e, j=j, m=m, xs=xs: e.tensor_tensor(
                            out=xs[:, j, :], in0=m[:, 0:NT], in1=xs[:, j, :], op=ALU.add),
                             reads=[bm, bx], writes=[bx])
                    else:
                        for half in range(2):
                            for kc in range(8):
                                P.op("pe", lambda e, kc=kc, j=j, half=half, s=s: e.matmul(
                                    mps[half][:, 0:NT],
                                    lhsT=wm[:, kc, half * D + j * 128: half * D + (j + 1) * 128],
                                    rhs=a_sb[s][:, kc, :], start=(kc == 0), stop=(kc == 7)),
                                     reads=[b_wm, b_a[s]], writes=[b_mps[half]], acc=(kc > 0))
                        P.op("act", lambda e: e.activation(out=sgm[:], in_=mps[1][:, 0:NT], func=AF.Sigmoid),
                             reads=[b_mps[1]], writes=[b_sgm])
                        P.op("dve", lambda e: e.tensor_tensor(out=tmpm[:], in0=mps[0][:, 0:NT], in1=sgm[:],
                                                              op=ALU.mult),
                             reads=[b_mps[0], b_sgm], writes=[b_tmpm])
                        P.op("dve", lambda e, j=j, xs=xs: e.tensor_tensor(
                            out=xs[:, j, :], in0=tmpm[:], in1=xs[:, j, :], op=ALU.add),
                             reads=[b_tmpm, bx], writes=[bx])
            if mode == "pool":
                emit_norm(P, C, xs[:], bx, gmix, b_gmix, hb[:, :, 16:16 + NT], b_hb, NT)
                W = 16 + NT
                for ct in range(8):
                    gi = ct // 2
                    w = 2 << gi
                    src = hb[:, ct, :]
                    bsrc = b_hb
                    cur = None
                    sh = 1
                    bufs2 = [(sA, b_sA), (sB, b_sB)]
                    k = 0
                    while sh < w:
                        dst, bdst = bufs2[k % 2]
                        a_in = src if cur is None else cur[0]
                        a_b = bsrc if cur is None else cur[1]
                        P.op("pool", lambda e, dst=dst, a_in=a_in, sh=sh: e.tensor_tensor(
                            out=dst[:, sh:W], in0=a_in[:, sh:W], in1=a_in[:, 0:W - sh], op=ALU.add),
                             reads=[a_b], writes=[bdst])
                        cur = (dst, bdst)
                        sh *= 2
                        k += 1
                    P.op("dve", lambda e, cur=cur, ct=ct, w=w: e.scalar_tensor_tensor(
                        out=diff[:, ct, :], in0=cur[0][:, 16:W], scalar=1.0 / w, in1=hb[:, ct, 16:W],
                        op0=ALU.mult, op1=ALU.subtract),
                         reads=[cur[1], b_hb], writes=[b_diff], acc=(ct > 0))
                    if tt == 0:
                        P.op("dve", lambda e, cur=cur, gi=gi: e.tensor_tensor(
                            out=fix[:], in0=cur[0][:, 16:32], in1=inv16[:, gi * 16:(gi + 1) * 16], op=ALU.mult),
                             reads=[cur[1], b_inv], writes=[b_fix])
                        P.op("dve", lambda e, ct=ct: e.tensor_tensor(
                            out=diff[:, ct, 0:16], in0=fix[:], in1=hb[:, ct, 16:32], op=ALU.subtract),
                             reads=[b_fix, b_hb, b_diff], writes=[b_diff], acc=True)
                for ct in range(8):
                    gi = ct // 2
                    m = mps[ct % 2]
                    bm = b_mps[ct % 2]
                    for kk in range(2):
                        P.op("pe", lambda e, kk=kk, gi=gi, ct=ct, m=m: e.matmul(
                            m[:, 0:NT], lhsT=pw[:, gi, kk, (ct % 2) * 128:(ct % 2 + 1) * 128],
                            rhs=diff[:, 2 * gi + kk, :], start=(kk == 0), stop=(kk == 1)),
                             reads=[b_pw, b_diff], writes=[bm], acc=(kk > 0))
                    P.op("dve", lambda e, ct=ct, m=m: e.tensor_scalar(
                        out=tmpm[:], in0=m[:, 0:NT], scalar1=pb[:, ct:ct + 1], scalar2=psc[:, ct:ct + 1],
                        op0=ALU.add, op1=ALU.mult),
                         reads=[bm, b_pb, b_psc], writes=[b_tmpm])
                    P.op("dve", lambda e, ct=ct, xs=xs: e.tensor_tensor(
                        out=xs[:, ct, :], in0=tmpm[:], in1=xs[:, ct, :], op=ALU.add),
                         reads=[b_tmpm, bx], writes=[bx])
                P.op("pool", lambda e: e.tensor_copy(out=hb[:, :, 0:16], in_=hb[:, :, NT:NT + 16]),
                     reads=[b_hb], writes=[b_hb])
            emit_ffn(P, C, F, xs[:], bx, NT)
            if final:
                emit_norm(P, C, xs[:], bx, gfin, b_gfin, fo[:], b_fo, NT)
                P.dma("sp", out_v[:, :, t0:t0 + NT], fo[:], b_fo, reads=[b_fo])
            else:
                P.dma("sp", out_v[:, :, t0:t0 + NT], xs[:], bx, reads=[bx])
        P.emit()
    return nc


def lay128(v):
    return np.ascontiguousarray(v.reshape(8, 128).T)


def run(nc, in_maps):
    res = run_bass_kernel_spmd(nc, in_maps, core_ids=list(range(NCORES)))
    return res.results


def build_fox(SEQ=S):
    nc = bass.Bass("TRN2", target_bir_lowering=False)
    NT = 512
    ntile = SEQ // NT
    nch = SEQ // 128
    HL = 8
    dt = lambda name, shape, dtype=F32, kind="ExternalInput": nc.dram_tensor(name, list(shape), dtype, kind=kind).ap()
    xT_d = dt("xT", [D, SEQ])
    gmix_d = dt("gmix", [128, 8])
    wqk_d = dt("wqk", [D, 1024])
    wv_d = dt("wv", [D, 512])
    wf_d = dt("wf", [D, 8])
    bf_d = dt("bf", [128, 8])
    tri_d = dt("tri", [128, 128])
    wf104_d = dt("wf104", [D, 104])
    bf104_d = dt("bf104", [104, 1])
    E_d = dt("Esel", [104, 8 * 128])
    out_d = dt("aT", [512, SEQ], BF16, kind="ExternalOutput")

    with ExitStack() as es:
        P = Prog(nc, es)
        C = Ctx(P, NT)
        wqk, b_wqk = load_weight_bf16(P, "wqk", wqk_d, 8, 1024, nsplit=2)
        wv, b_wv = load_weight_bf16(P, "wv", wv_d, 8, 512, nsplit=1)
        wf, b_wf = load_weight_bf16(P, "wf", wf_d, 8, 8, nsplit=1)
        gmix, b_gmix = load_small(P, "gmix", gmix_d, [128, 8])
        bfb, b_bfb = load_small(P, "bf", bf_d, [128, 8])
        tri32, b_tri32 = load_small(P, "tri32", tri_d, [128, 128])
        wf104, b_wf104 = load_weight_bf16(P, "wf104", wf104_d, 8, 104, nsplit=1)
        bf104, b_bf104 = load_small(P, "bf104", bf104_d, [104, 1])
        negbf = P.sb("negbf", [104, 1], F32)
        b_negbf = P.buf("negbf")
        P.op("dve", lambda e: e.tensor_scalar(out=negbf[:], in0=bf104[:], scalar1=-1.0, scalar2=None, op0=ALU.mult),
             reads=[b_bf104], writes=[b_negbf])
        Eb = P.sb("Eb", [104, 8 * 128], BF16)
        b_Eb = P.buf("Eb")
        P.dma("pool", Eb[:], E_d, b_Eb, writes=[b_Eb])
        eT = P.sb("eT", [104, SEQ], F32)
        b_eT = P.buf("eT")
        dqb = P.sb("dqb", [104, SEQ], BF16)
        b_dqb = P.buf("dqb")
        ones512 = P.sb("ones512", [104, 512], F32)
        b_ones512 = P.buf("ones512")
        P.op("pool", lambda e: e.memset(ones512[:], 1.0), writes=[b_ones512])
        tribf = P.sb("tribf", [128, 128], BF16)
        b_tribf = P.buf("tribf")
        P.op("dve", lambda e: e.tensor_copy(out=tribf[:], in_=tri32[:]), reads=[b_tri32], writes=[b_tribf])
        ones32 = P.sb("ones32", [128, 128], F32)
        b_ones32 = P.buf("ones32")
        P.op("pool", lambda e: e.memset(ones32[:], 1.0), writes=[b_ones32])
        sel = P.sb("sel65", [65, 64], F32)
        b_sel = P.buf("sel65")
        P.op("pool", lambda e: e.memset(sel[:], 0.0), writes=[b_sel])
        P.op("pool", lambda e: e.memset(sel[64:65, :], 1.0), writes=[b_sel], acc=True)

        qT = P.sb("qT", [128, 4, SEQ], BF16)
        kT = P.sb("kT", [128, 4, SEQ], BF16)
        b_qT = P.buf("qT")
        b_kT = P.buf("kT")
        vaug = P.sb("vaug", [128, nch, HL * 65], BF16)
        b_v = P.buf("vaug")
        P.op("pool", lambda e: e.memset(vaug[:], 1.0), writes=[b_v])
        fsb = P.sb("fsb", [128, nch * 8], F32)
        b_f = P.buf("fsb")
        x_sb = P.sb("x_sb", [128, 8, NT], F32)
        b_x = P.buf("x_sb")
        h = P.sb("h", [128, 8, NT], BF16)
        b_h = P.buf("h")
        pj = [P.ps(f"pj{i}", [128, 512]) for i in range(2)]
        b_pj = [P.buf(f"pj{i}") for i in range(2)]
        xT_v = xT_d.rearrange("(kc p) t -> p kc t", p=128)
        pjc = [0]

        def nextpj():
            s = pjc[0] % 2
            pjc[0] += 1
            return pj[s], b_pj[s]

        for tt in range(ntile):
            t0 = tt * NT
            P.dma("sp", x_sb[:], xT_v[:, :, t0:t0 + NT], b_x, writes=[b_x])
            emit_norm(P, C, x_sb[:], b_x, gmix, b_gmix, h[:], b_h, NT)
            for c in range(8):
                ps, bps = nextpj()
                for kc in range(8):
                    P.op("pe", lambda e, kc=kc, c=c, ps=ps: e.matmul(
                        ps[:, :], lhsT=wqk[:, kc, c * 128:(c + 1) * 128], rhs=h[:, kc, :],
                        start=(kc == 0), stop=(kc == 7)),
                         reads=[b_wqk, b_h], writes=[bps], acc=(kc > 0))
                dst = qT if c < 4 else kT
                bdst = b_qT if c < 4 else b_kT
                P.op("act", lambda e, dst=dst, c=c, ps=ps, t0=t0: e.activation(
                    out=dst[:, c % 4, t0:t0 + NT], in_=ps[:, :], func=AF.Copy),
                     reads=[bps], writes=[bdst], acc=True)
            ps, bps = nextpj()
            for kc in range(8):
                P.op("pe", lambda e, kc=kc, ps=ps: e.matmul(
                    ps[0:104, :], lhsT=wf104[:, kc, :], rhs=h[:, kc, :], start=(kc == 0), stop=(kc == 7)),
                     reads=[b_wf104, b_h], writes=[bps], acc=(kc > 0))
            P.op("act", lambda e, ps=ps, t0=t0: e.activation(
                out=eT[:, t0:t0 + NT], in_=ps[0:104, :], func=AF.Exp, bias=negbf[:, 0:1], scale=-1.0),
                 reads=[bps, b_negbf], writes=[b_eT], acc=True)
            for sub in range(NT // 128):
                j = tt * (NT // 128) + sub
                ps, bps = nextpj()
                for kc in range(8):
                    P.op("pe", lambda e, kc=kc, sub=sub, ps=ps: e.matmul(
                        ps[:, :], lhsT=h[:, kc, sub * 128:(sub + 1) * 128], rhs=wv[:, kc, :],
                        start=(kc == 0), stop=(kc == 7)),
                         reads=[b_wv, b_h], writes=[bps], acc=(kc > 0))
                P.op("dve", lambda e, j=j, ps=ps: e.tensor_copy(
                    out=vaug[:, j, :].rearrange("p (h d) -> p h d", d=65)[:, :, 0:64],
                    in_=ps[:, :].rearrange("p (h d) -> p h d", d=64)),
                     reads=[bps], writes=[b_v], acc=True)
                ps, bps = nextpj()
                for kc in range(8):
                    P.op("pe", lambda e, kc=kc, sub=sub, ps=ps: e.matmul(
                        ps[:, 0:8], lhsT=h[:, kc, sub * 128:(sub + 1) * 128], rhs=wf[:, kc, :],
                        start=(kc == 0), stop=(kc == 7)),
                         reads=[b_wf, b_h], writes=[bps], acc=(kc > 0))
                P.op("dve", lambda e, j=j, ps=ps: e.tensor_tensor(
                    out=fsb[:, j * 8:(j + 1) * 8], in0=ps[:, 0:8], in1=bfb[:], op=ALU.add),
                     reads=[bps, b_bfb], writes=[b_f], acc=True)

        NF = nch * 8
        lsb = P.sb("lsb", [128, NF], F32)
        b_l = P.buf("lsb")
        P.op("act", lambda e: e.activation(out=lsb[:], in_=fsb[:], func=AF.Exp, scale=-1.0),
             reads=[b_f], writes=[b_l])
        P.op("act", lambda e: e.activation(out=lsb[:], in_=lsb[:], func=AF.Ln, bias=1.0),
             reads=[b_l], writes=[b_l])
        ps_c, bps_c = nextpj()
        ps_t, bps_t = nextpj()
        P.op("pe", lambda e: e.matmul(ps_c[:, 0:NF], lhsT=tri32[:], rhs=lsb[:], start=True, stop=True),
             reads=[b_tri32, b_l], writes=[bps_c])
        P.op("pe", lambda e: e.matmul(ps_t[:, 0:NF], lhsT=ones32[:], rhs=lsb[:], start=True, stop=True),
             reads=[b_ones32, b_l], writes=[bps_t])
        pre = P.sb("pre", [128, (nch + 1) * 8], F32)
        b_pre = P.buf("pre")
        tot = P.sb("tot", [128, NF], F32)
        b_tot = P.buf("tot")
        P.op("dve", lambda e: e.tensor_copy(out=tot[:], in_=ps_t[:, 0:NF]), reads=[bps_t], writes=[b_tot])
        P.op("pool", lambda e: e.memset(pre[:, 0:8], 0.0), writes=[b_pre])
        for j in range(nch):
            P.op("dve", lambda e, j=j: e.tensor_tensor(
                out=pre[:, (j + 1) * 8:(j + 2) * 8], in0=pre[:, j * 8:(j + 1) * 8], in1=tot[:, j * 8:(j + 1) * 8],
                op=ALU.add), reads=[b_pre, b_tot], writes=[b_pre])
        cpos = P.sb("cpos", [128, NF], F32)
        b_cpos = P.buf("cpos")
        P.op("dve", lambda e: e.tensor_tensor(out=cpos[:], in0=ps_c[:, 0:NF], in1=pre[:, 0:NF], op=ALU.add),
             reads=[bps_c, b_pre], writes=[b_cpos])
        ngrp = SEQ // 512
        bias = P.sb("bias", [128, ngrp, NF], F32)
        b_bias = P.buf("bias")
        first = True
        for g in range(ngrp):
            nj = 4 * g + 4
            for hl in range(HL):
                P.op("dve", lambda e, g=g, hl=hl, nj=nj: e.tensor_scalar(
                    out=bias[:, g, :].rearrange("p (j h) -> p j h", h=8)[:, 0:nj, hl],
                    in0=cpos[:].rearrange("p (j h) -> p j h", h=8)[:, 0:nj, hl],
                    scalar1=pre[:, (4 * g + 4) * 8 + hl:(4 * g + 4) * 8 + hl + 1], scalar2=None,
                    op0=ALU.subtract),
                     reads=[b_cpos, b_pre], writes=[b_bias], acc=not first)
                first = False

        P.op("act", lambda e: e.activation(out=eT[:], in_=eT[:], func=AF.Ln, bias=1.0), reads=[b_eT], writes=[b_eT])
        cposT = x_sb[:].rearrange("p a b -> p (a b)")
        for g in range(ngrp):
            P.op("dve", lambda e, g=g: e.tensor_tensor_scan(
                out=cposT[0:104, g * 512:(g + 1) * 512], data0=ones512[:], data1=eT[:, g * 512:(g + 1) * 512],
                initial=(0.0 if g == 0 else cposT[0:104, g * 512 - 1:g * 512]), op0=ALU.mult, op1=ALU.add),
                 reads=[b_ones512, b_eT, b_x], writes=[b_x])
        dq8 = P.sb("dq8", [104, 512], F32)
        b_dq8 = P.buf("dq8")
        hi32 = P.sb("hi32", [104, 512], F32)
        b_hi32 = P.buf("hi32")
        for g in range(ngrp):
            gs = slice(g * 512, (g + 1) * 512)
            P.op("dve", lambda e, g=g, gs=gs: e.tensor_scalar(
                out=dq8[:], in0=cposT[0:104, gs], scalar1=cposT[0:104, (g + 1) * 512 - 1:(g + 1) * 512], scalar2=-8.0,
                op0=ALU.subtract, op1=ALU.mult), reads=[b_x], writes=[b_dq8])
            P.op("dve", lambda e, gs=gs: e.tensor_copy(out=dqb[:, gs], in_=dq8[:]), reads=[b_dq8], writes=[b_dqb],
                 acc=(g > 0))
            P.op("dve", lambda e, gs=gs: e.tensor_copy(out=hi32[:], in_=dqb[:, gs]), reads=[b_dqb], writes=[b_hi32])
            P.op("dve", lambda e: e.tensor_tensor(out=dq8[:], in0=dq8[:], in1=hi32[:], op=ALU.subtract),
                 reads=[b_dq8, b_hi32], writes=[b_dq8])
            P.op("dve", lambda e, gs=gs: e.tensor_copy(out=dqb[32:40, gs], in_=dq8[32:40, :]), reads=[b_dq8],
                 writes=[b_dqb], acc=True)
            P.op("dve", lambda e, gs=gs: e.tensor_copy(out=dqb[96:104, gs], in_=dq8[96:104, :]), reads=[b_dq8],
                 writes=[b_dqb], acc=True)

        NS = 3
        S_ps = [P.ps(f"S_ps{i}", [128, 512]) for i in range(NS)]
        b_S = [P.buf(f"S_ps{i}") for i in range(NS)]
        O_ps = [P.ps(f"O_ps{i}", [128, 512]) for i in range(2)]
        b_O = [P.buf(f"O_ps{i}") for i in range(2)]
        NPB = 4
        Pb = [P.sb(f"Pb{i}", [128, 512], BF16) for i in range(NPB)]
        b_Pb = [P.buf(f"Pb{i}") for i in range(NPB)]
        o_sb = [wqk[:, i, :].bitcast(F32) for i in range(2)]
        b_o = [P.buf(f"o_sb{i}") for i in range(2)]
        rec = [wqk[:, 2 + i, :].bitcast(F32) for i in range(2)]
        b_rec = [P.buf(f"rec{i}") for i in range(2)]
        ast = [wqk[:, 4 + i, 0:512] for i in range(2)]
        b_ast = [P.buf(f"ast{i}") for i in range(2)]
        P.op("dve", lambda e: e.memset(o_sb[0][0:65, :], 0.0), writes=[b_wqk] + b_o + b_rec + b_ast)
        it = 0
        gi = 0
        for hl in range(HL):
            cq = hl // 2
            pb = (hl % 2) * 64
            for g in range(ngrp):
                q0 = g * 512
                nkb = 4 * g + 4
                oi = gi % 2
                gi += 1
                for j in range(nkb):
                    jj = j - 4 * g
                    off = 128 * jj if jj > 0 else 0
                    N = 512 - off
                    si = it % NS
                    pi = it % NPB
                    it += 1
                    P.op("pe", lambda e, j=j, si=si, off=off, N=N, cq=cq, pb=pb, q0=q0: e.matmul(
                        S_ps[si][:, 0:N], lhsT=kT[pb:pb + 64, cq, j * 128:(j + 1) * 128],
                        rhs=qT[pb:pb + 64, cq, q0 + off:q0 + 512], start=True, stop=False),
                         reads=[b_kT, b_qT], writes=[b_S[si]])
                    P.op("pe", lambda e, si=si, off=off, N=N, pb=pb, q0=q0, hl=hl: e.matmul(
                        S_ps[si][:, 0:N], lhsT=Eb[pb:pb + 40, hl * 128:(hl + 1) * 128],
                        rhs=dqb[pb:pb + 40, q0 + off:q0 + 512], start=False, stop=True),
                         reads=[b_Eb, b_dqb], writes=[b_S[si]], acc=True)
                    P.op("act", lambda e, j=j, si=si, pi=pi, N=N, g=g, hl=hl: e.activation(
                        out=Pb[pi][:, 0:N], in_=S_ps[si][:, 0:N], func=AF.Exp,
                        bias=bias[:, g, j * 8 + hl:j * 8 + hl + 1], scale=0.125),
                         reads=[b_S[si], b_bias], writes=[b_Pb[pi]])
                    if jj >= 0:
                        P.op("pool", lambda e, pi=pi: e.tensor_tensor(
                            out=Pb[pi][:, 0:128], in0=Pb[pi][:, 0:128], in1=tribf[:], op=ALU.mult),
                             reads=[b_Pb[pi], b_tribf], writes=[b_Pb[pi]])
                    P.op("pe", lambda e, j=j, pi=pi, oi=oi, off=off, N=N, hl=hl, nkb=nkb: e.matmul(
                        O_ps[oi][0:65, off:512], lhsT=vaug[:, j, hl * 65:(hl + 1) * 65], rhs=Pb[pi][:, 0:N],
                        start=(j == 0), stop=(j == nkb - 1)),
                         reads=[b_v, b_Pb[pi]], writes=[b_O[oi]], acc=(j > 0))
                P.op("dve", lambda e, oi=oi: e.tensor_copy(out=o_sb[oi][0:65, :], in_=O_ps[oi][0:65, :]),
                     reads=[b_O[oi]], writes=[b_o[oi]])
                dps, bdps = nextpj()
                P.op("pe", lambda e, oi=oi, dps=dps: e.matmul(dps[0:64, :], lhsT=sel[:], rhs=o_sb[oi][0:65, :],
                                                            start=True, stop=True),
                     reads=[b_sel, b_o[oi]], writes=[bdps])
                P.op("dve", lambda e, oi=oi, dps=dps: e.reciprocal(out=rec[oi][0:64, :], in_=dps[0:64, :]),
                     reads=[bdps], writes=[b_rec[oi]])
                P.op("dve", lambda e, oi=oi: e.tensor_tensor(out=ast[oi][0:64, :], in0=o_sb[oi][0:64, :],
                                                            in1=rec[oi][0:64, :], op=ALU.mult),
                     reads=[b_o[oi], b_rec[oi]], writes=[b_ast[oi]])
                P.dma("sp", out_d[hl * 64:(hl + 1) * 64, q0:q0 + 512], ast[oi][0:64, :], b_ast[oi],
                      reads=[b_ast[oi]])
        P.emit()
    return nc


TWO_PI = 6.283185307179586
PI = 3.141592653589793


def build_s5(SEQ=S, stage=9):
    nc = bass.Bass("TRN2", target_bir_lowering=False)
    NT = 256
    ntile = SEQ // NT
    NQ = 16
    W = NT + 1
    dt = lambda name, shape, dtype=F32, kind="ExternalInput": nc.dram_tensor(name, list(shape), dtype, kind=kind).ap()
    xT_d = dt("xT", [D, SEQ])
    gmix_d = dt("gmix", [128, 8])
    win_d = dt("win", [D, 512])
    Bb_d = dt("Bblk", [128, 32, 128])
    Cb_d = dt("Cblk", [128, 32, 128])
    are_d = dt("are", [128, NQ])
    aim_d = dt("aim", [128, NQ])
    ldt_d = dt("ldt", [128, NQ])
    dsk_d = dt("dsk", [128, 4])
    iota_d = dt("iota", [128, W])
    out_d = dt("gT", [512, SEQ], BF16, kind="ExternalOutput")

    with ExitStack() as es:
        P = Prog(nc, es)
        C = Ctx(P, NT)
        win, b_win = load_weight_bf16(P, "win", win_d, 8, 512, nsplit=1)
        Bb = P.sb("Bb", [128, 32, 128], BF16)
        b_Bb = P.buf("Bb")
        P.dma("pool", Bb[:], Bb_d, b_Bb, writes=[b_Bb])
        Cb = P.sb("Cb", [128, 32, 128], BF16)
        b_Cb = P.buf("Cb")
        P.dma("pool", Cb[:], Cb_d, b_Cb, writes=[b_Cb])
        for q in range(NQ):
            P.op("dve", lambda e, q=q: e.tensor_scalar(out=Cb[:, 2 * q + 1, :], in0=Cb[:, 2 * q + 1, :], scalar1=-1.0,
                                                       scalar2=None, op0=ALU.mult), reads=[b_Cb], writes=[b_Cb])
        gmix, b_gmix = load_small(P, "gmix", gmix_d, [128, 8])
        are, b_are = load_small(P, "are", are_d, [128, NQ])
        aim, b_aim = load_small(P, "aim", aim_d, [128, NQ])
        ldt, b_ldt = load_small(P, "ldt", ldt_d, [128, NQ])
        dsk, b_dsk = load_small(P, "dsk", dsk_d, [128, 4])
        iota, b_iota = load_small(P, "iota", iota_d, [128, W])

        cnt = [0]

        def tmp(shape, dtype=F32):
            cnt[0] += 1
            return P.sb(f"t{cnt[0]}", shape, dtype), P.buf(f"t{cnt[0]}")

        def dve(fn, reads, writes, acc=False):
            P.op("dve", fn, reads=reads, writes=writes, acc=acc)

        def pool(fn, reads, writes, acc=False):
            P.op("pool", fn, reads=reads, writes=writes, acc=acc)

        dtv, b_dtv = tmp([128, NQ])
        P.op("act", lambda e: e.activation(out=dtv[:], in_=ldt[:], func=AF.Exp), reads=[b_ldt], writes=[b_dtv])
        ar, b_ar = tmp([128, NQ])
        th, b_th = tmp([128, NQ])
        dve(lambda e: e.tensor_tensor(out=ar[:], in0=are[:], in1=dtv[:], op=ALU.mult), [b_are, b_dtv], [b_ar])
        dve(lambda e: e.tensor_tensor(out=th[:], in0=aim[:], in1=dtv[:], op=ALU.mult), [b_aim, b_dtv], [b_th])
        rr, b_rr = tmp([128, NQ])
        P.op("act", lambda e: e.activation(out=rr[:], in_=ar[:], func=AF.Exp), reads=[b_ar], writes=[b_rr])

        MAGIC = 12582912.0

        def emit_sin(x_ap, b_x, out_ap, b_out, n, shift, scr):
            (ta, b_ta), (ti, b_ti), (tb, b_tb) = scr
            dve(lambda e: e.tensor_scalar(out=ta[:, 0:n], in0=x_ap, scalar1=shift, scalar2=1.0 / TWO_PI,
                                          op0=ALU.add, op1=ALU.mult), [b_x], [b_ta])
            dve(lambda e: e.tensor_scalar(out=ti[:, 0:n], in0=ta[:, 0:n], scalar1=MAGIC, scalar2=None,
                                          op0=ALU.add), [b_ta], [b_ti])
            dve(lambda e: e.tensor_scalar(out=ta[:, 0:n], in0=ti[:, 0:n], scalar1=-MAGIC, scalar2=None,
                                          op0=ALU.add), [b_ti], [b_ta])
            dve(lambda e: e.scalar_tensor_tensor(out=tb[:, 0:n], in0=ta[:, 0:n], scalar=-TWO_PI, in1=x_ap,
                                                 op0=ALU.mult, op1=ALU.add), [b_ta, b_x], [b_tb])
            dve(lambda e: e.tensor_scalar(out=ta[:, 0:n], in0=tb[:, 0:n], scalar1=shift, scalar2=PI,
                                          op0=ALU.add, op1=ALU.min), [b_tb], [b_ta])
            dve(lambda e: e.tensor_scalar(out=tb[:, 0:n], in0=ta[:, 0:n], scalar1=-PI, scalar2=None,
                                          op0=ALU.max), [b_ta], [b_tb])
            P.op("act", lambda e: e.activation(out=out_ap, in_=tb[:, 0:n], func=AF.Sin), reads=[b_tb],
                 writes=[b_out])

        scr = (tmp([128, W]), tmp([128, W]), tmp([128, W]))
        c1, b_c1 = tmp([128, NQ])
        s1, b_s1 = tmp([128, NQ])
        emit_sin(th[:], b_th, s1[:], b_s1, NQ, 0.0, scr)
        emit_sin(th[:], b_th, c1[:], b_c1, NQ, PI / 2, scr)
        nre, b_nre = tmp([128, NQ])
        nim, b_nim = tmp([128, NQ])
        dve(lambda e: e.tensor_tensor(out=nre[:], in0=rr[:], in1=c1[:], op=ALU.mult), [b_rr, b_c1], [b_nre])
        dve(lambda e: e.tensor_scalar(out=nre[:], in0=nre[:], scalar1=-1.0, scalar2=None, op0=ALU.add), [b_nre], [b_nre])
        dve(lambda e: e.tensor_tensor(out=nim[:], in0=rr[:], in1=s1[:], op=ALU.mult), [b_rr, b_s1], [b_nim])
        den, b_den = tmp([128, NQ])
        t16, b_t16 = tmp([128, NQ])
        dve(lambda e: e.tensor_tensor(out=den[:], in0=are[:], in1=are[:], op=ALU.mult), [b_are], [b_den])
        dve(lambda e: e.tensor_tensor(out=t16[:], in0=aim[:], in1=aim[:], op=ALU.mult), [b_aim], [b_t16])
        dve(lambda e: e.tensor_tensor(out=den[:], in0=den[:], in1=t16[:], op=ALU.add), [b_den, b_t16], [b_den])
        dve(lambda e: e.reciprocal(out=den[:], in_=den[:]), [b_den], [b_den])
        cre, b_cre = tmp([128, NQ])
        cim, b_cim = tmp([128, NQ])
        ncre, b_ncre = tmp([128, NQ])
        dve(lambda e: e.tensor_tensor(out=cre[:], in0=nre[:], in1=are[:], op=ALU.mult), [b_nre, b_are], [b_cre])
        dve(lambda e: e.tensor_tensor(out=t16[:], in0=nim[:], in1=aim[:], op=ALU.mult), [b_nim, b_aim], [b_t16])
        dve(lambda e: e.tensor_tensor(out=cre[:], in0=cre[:], in1=t16[:], op=ALU.add), [b_cre, b_t16], [b_cre])
        dve(lambda e: e.tensor_tensor(out=cre[:], in0=cre[:], in1=den[:], op=ALU.mult), [b_cre, b_den], [b_cre])
        dve(lambda e: e.tensor_tensor(out=cim[:], in0=nim[:], in1=are[:], op=ALU.mult), [b_nim, b_are], [b_cim])
        dve(lambda e: e.tensor_tensor(out=t16[:], in0=nre[:], in1=aim[:], op=ALU.mult), [b_nre, b_aim], [b_t16])
        dve(lambda e: e.tensor_tensor(out=cim[:], in0=cim[:], in1=t16[:], op=ALU.subtract), [b_cim, b_t16], [b_cim])
        dve(lambda e: e.tensor_tensor(out=cim[:], in0=cim[:], in1=den[:], op=ALU.mult), [b_cim, b_den], [b_cim])
        dve(lambda e: e.tensor_scalar(out=ncre[:], in0=cre[:], scalar1=-1.0, scalar2=None, op0=ALU.mult),
            [b_cre], [b_ncre])

        cosT = P.sb("cosT", [128, NQ, W], F32)
        sinT = P.sb("sinT", [128, NQ, W], F32)
        rcre = P.sb("rcre", [128, NQ, W], F32)
        rcim = P.sb("rcim", [128, NQ, W], F32)
        b_tab = P.buf("tables")
        cN, b_cN = tmp([128, NQ])
        sN, b_sN = tmp([128, NQ])
        ang, b_ang = tmp([128, W])
        cosf, b_cosf = tmp([128, W])
        sinf, b_sinf = tmp([128, W])
        tq, b_tq = tmp([128, W])
        first = True
        for q in range(NQ if (stage >= 1 and stage != 4) else 0):
            dve(lambda e, q=q: e.tensor_scalar(out=ang[:], in0=iota[:], scalar1=th[:, q:q + 1], scalar2=None,
                                               op0=ALU.mult), [b_iota, b_th], [b_ang])
            emit_sin(ang[:], b_ang, sinf[:], b_sinf, W, 0.0, scr)
            emit_sin(ang[:], b_ang, cosf[:], b_cosf, W, PI / 2, scr)
            pool(lambda e, q=q: e.tensor_copy(out=cosT[:, q, :], in_=cosf[:]), [b_cosf], [b_tab], acc=not first)
            first = False
            pool(lambda e, q=q: e.tensor_copy(out=sinT[:, q, :], in_=sinf[:]), [b_sinf], [b_tab], acc=True)
            pool(lambda e, q=q: e.tensor_copy(out=cN[:, q:q + 1], in_=cosf[:, NT:NT + 1]), [b_cosf], [b_cN], acc=(q > 0))
            pool(lambda e, q=q: e.tensor_copy(out=sN[:, q:q + 1], in_=sinf[:, NT:NT + 1]), [b_sinf], [b_sN], acc=(q > 0))
            dve(lambda e, q=q: e.tensor_scalar(out=tq[:], in0=cosf[:], scalar1=cre[:, q:q + 1], scalar2=None,
                                               op0=ALU.mult), [b_cosf, b_cre], [b_tq])
            dve(lambda e, q=q: e.scalar_tensor_tensor(out=rcre[:, q, :], in0=sinf[:], scalar=cim[:, q:q + 1],
                                                      in1=tq[:], op0=ALU.mult, op1=ALU.add),
                [b_sinf, b_cim, b_tq], [b_tab], acc=True)
            dve(lambda e, q=q: e.tensor_scalar(out=tq[:], in0=cosf[:], scalar1=cim[:, q:q + 1], scalar2=None,
                                               op0=ALU.mult), [b_cosf, b_cim], [b_tq])
            dve(lambda e, q=q: e.scalar_tensor_tensor(out=rcim[:, q, :], in0=sinf[:], scalar=ncre[:, q:q + 1],
                                                      in1=tq[:], op0=ALU.mult, op1=ALU.add),
                [b_sinf, b_ncre, b_tq], [b_tab], acc=True)

        pool = dve
        x_sb = P.sb("x_sb", [128, 8, NT], F32)
        b_x = P.buf("x_sb")
        h = P.sb("h", [128, 8, NT], BF16)
        b_h = P.buf("h")
        u32 = P.sb("u32", [128, 4, NT], F32)
        b_u32 = P.buf("u32")
        ubf = P.sb("ubf", [128, 4, NT], BF16)
        b_ubf = P.buf("ubf")
        pj = [P.ps(f"pj{i}", [128, 512]) for i in range(2)]
        b_pj = [P.buf(f"pj{i}") for i in range(2)]
        A_ps = [P.ps(f"A_ps{i}", [128, 512]) for i in range(2)]
        B_ps = [P.ps(f"B_ps{i}", [128, 512]) for i in range(2)]
        b_A = [P.buf(f"A_ps{i}") for i in range(2)]
        b_B = [P.buf(f"B_ps{i}") for i in range(2)]
        Y_ps = P.ps("Y_ps", [128, 512])
        b_Y = P.buf("Y_ps")
        t1, b_t1 = tmp([128, NT]); t2, b_t2 = tmp([128, NT]); t3, b_t3 = tmp([128, NT]); t4, b_t4 = tmp([128, NT])
        zre, b_zre = tmp([128, NT]); zim, b_zim = tmp([128, NT])
        wre = [tmp([128, NT]) for _ in range(2)]
        wim = [tmp([128, NT]) for _ in range(2)]
        p1, b_p1 = tmp([128, NT]); p2, b_p2 = tmp([128, NT])
        sst = [P.sb(f"sst{i}", [128, 8, NT], BF16) for i in range(2)]
        b_sst = [P.buf(f"sst{i}") for i in range(2)]
        wl_re, b_wlre = tmp([128, NQ]); wl_im, b_wlim = tmp([128, NQ])
        in_re, b_inre = tmp([128, NQ]); in_im, b_inim = tmp([128, NQ])
        i1, b_i1 = tmp([128, NQ]); i2, b_i2 = tmp([128, NQ])
        rfill, b_rfill = tmp([128, NT])
        onesf, b_onesf = tmp([128, NT])
        pool(lambda e: e.memset(onesf[:], 1.0), [], [b_onesf])
        pool(lambda e: e.memset(in_re[:], 0.0), [], [b_inre])
        pool(lambda e: e.memset(in_im[:], 0.0), [], [b_inim])
        ysb, b_ysb = tmp([128, NT]); g1, b_g1 = t1, b_t1; g2, b_g2 = t2, b_t2; sg, b_sg = t3, b_t3
        gout = P.sb("gout", [128, 4, NT], BF16)
        b_gout = P.buf("gout")
        xT_v = xT_d.rearrange("(kc p) t -> p kc t", p=128)
        out_v = out_d.rearrange("(ct p) t -> p ct t", p=128)
        qi = 0
        for tt in range(ntile if stage >= 2 else 0):
            t0 = tt * NT
            P.dma("sp", x_sb[:], xT_v[:, :, t0:t0 + NT], b_x, writes=[b_x])
            emit_norm(P, C, x_sb[:], b_x, gmix, b_gmix, h[:], b_h, NT)
            for ct in range(4):
                ps, bps = pj[ct % 2], b_pj[ct % 2]
                for kc in range(8):
                    P.op("pe", lambda e, kc=kc, ct=ct, ps=ps: e.matmul(
                        ps[:, 0:NT], lhsT=win[:, kc, ct * 128:(ct + 1) * 128], rhs=h[:, kc, :],
                        start=(kc == 0), stop=(kc == 7)), reads=[b_win, b_h], writes=[bps], acc=(kc > 0))
                P.op("act", lambda e, ct=ct, ps=ps: e.activation(out=u32[:, ct, :], in_=ps[:, 0:NT], func=AF.Copy),
                     reads=[bps], writes=[b_u32], acc=(ct > 0))
                dve(lambda e, ct=ct: e.tensor_copy(out=ubf[:, ct, :], in_=u32[:, ct, :]), [b_u32], [b_ubf], acc=(ct > 0))
            if stage < 5:
                continue
            for ct in range(4):
                ss_i = ct % 2
                for st in range(4):
                    q = ct * 4 + st
                    ai = qi % 2
                    qi += 1
                    P.op("pe", lambda e, q=q, ct=ct, ai=ai: e.matmul(
                        A_ps[ai][:, 0:NT], lhsT=Bb[:, 2 * q, :], rhs=ubf[:, ct, :], start=True, stop=True),
                         reads=[b_Bb, b_ubf], writes=[b_A[ai]])
                    P.op("pe", lambda e, q=q, ct=ct, ai=ai: e.matmul(
                        B_ps[ai][:, 0:NT], lhsT=Bb[:, 2 * q + 1, :], rhs=ubf[:, ct, :], start=True, stop=True),
                         reads=[b_Bb, b_ubf], writes=[b_B[ai]])
                    dve(lambda e, q=q, ai=ai: e.tensor_tensor(out=t1[:], in0=A_ps[ai][:, 0:NT], in1=rcre[:, q, 0:NT],
                                                             op=ALU.mult), [b_A[ai], b_tab], [b_t1])
                    dve(lambda e, q=q, ai=ai: e.tensor_tensor(out=t2[:], in0=B_ps[ai][:, 0:NT], in1=rcim[:, q, 0:NT],
                                                             op=ALU.mult), [b_B[ai], b_tab], [b_t2])
                    dve(lambda e, q=q, ai=ai: e.tensor_tensor(out=t3[:], in0=A_ps[ai][:, 0:NT], in1=rcim[:, q, 0:NT],
                                                             op=ALU.mult), [b_A[ai], b_tab], [b_t3])
                    dve(lambda e, q=q, ai=ai: e.tensor_tensor(out=t4[:], in0=B_ps[ai][:, 0:NT], in1=rcre[:, q, 0:NT],
                                                             op=ALU.mult), [b_B[ai], b_tab], [b_t4])
                    pool(lambda e: e.tensor_tensor(out=zre[:], in0=t1[:], in1=t2[:], op=ALU.subtract),
                         [b_t1, b_t2], [b_zre])
                    pool(lambda e: e.tensor_tensor(out=zim[:], in0=t3[:], in1=t4[:], op=ALU.add),
                         [b_t3, b_t4], [b_zim])
                    if stage < 6:
                        continue
                    (wr, b_wr), (wi, b_wi) = wre[ai], wim[ai]
                    pool(lambda e, q=q: e.tensor_scalar(out=rfill[:], in0=onesf[:], scalar1=rr[:, q:q + 1], scalar2=None,
                                                        op0=ALU.mult), [b_onesf, b_rr], [b_rfill])
                    if stage >= 3:
                        dve(lambda e, q=q, wr=wr: e.tensor_tensor_scan(
                            out=wr[:], data0=rfill[:], data1=zre[:],
                            initial=in_re[:, q:q + 1], op0=ALU.mult, op1=ALU.add),
                            [b_rfill, b_zre, b_inre], [b_wr])
                        dve(lambda e, q=q, wi=wi: e.tensor_tensor_scan(
                            out=wi[:], data0=rfill[:], data1=zim[:],
                            initial=in_im[:, q:q + 1], op0=ALU.mult, op1=ALU.add),
                            [b_rfill, b_zim, b_inim], [b_wi])
                    else:
                        dve(lambda e, wr=wr: e.tensor_copy(out=wr[:], in_=zre[:]), [b_zre], [b_wr])
                        dve(lambda e, wi=wi: e.tensor_copy(out=wi[:], in_=zim[:]), [b_zim], [b_wi])
                    pool(lambda e, q=q, wr=wr: e.tensor_copy(out=wl_re[:, q:q + 1], in_=wr[:, NT - 1:NT]),
                         [b_wr], [b_wlre], acc=True)
                    pool(lambda e, q=q, wi=wi: e.tensor_copy(out=wl_im[:, q:q + 1], in_=wi[:, NT - 1:NT]),
                         [b_wi], [b_wlim], acc=True)
                    pool(lambda e, q=q, wr=wr: e.tensor_tensor(out=p1[:], in0=wr[:], in1=cosT[:, q, 0:NT], op=ALU.mult),
                         [b_wr, b_tab], [b_p1])
                    pool(lambda e, q=q, wi=wi: e.tensor_tensor(out=p2[:], in0=wi[:], in1=sinT[:, q, 0:NT], op=ALU.mult),
                         [b_wi, b_tab], [b_p2])
                    pool(lambda e, st=st, ss_i=ss_i: e.tensor_tensor(out=sst[ss_i][:, 2 * st, :], in0=p1[:], in1=p2[:],
                                                                    op=ALU.subtract),
                         [b_p1, b_p2], [b_sst[ss_i]], acc=(st > 0))
                    pool(lambda e, q=q, wr=wr: e.tensor_tensor(out=p1[:], in0=wr[:], in1=sinT[:, q, 0:NT], op=ALU.mult),
                         [b_wr, b_tab], [b_p1])
                    pool(lambda e, q=q, wi=wi: e.tensor_tensor(out=p2[:], in0=wi[:], in1=cosT[:, q, 0:NT], op=ALU.mult),
                         [b_wi, b_tab], [b_p2])
                    pool(lambda e, st=st, ss_i=ss_i: e.tensor_tensor(out=sst[ss_i][:, 2 * st + 1, :], in0=p1[:], in1=p2[:],
                                                                    op=ALU.add),
                         [b_p1, b_p2], [b_sst[ss_i]], acc=True)
                if stage < 7:
                    continue
                for k in range(8):
                    q = ct * 4 + k // 2
                    P.op("pe", lambda e, k=k, q=q, ss_i=ss_i: e.matmul(
                        Y_ps[:, 0:NT], lhsT=Cb[:, 2 * q + (k % 2), :], rhs=sst[ss_i][:, k, :],
                        start=(k == 0), stop=(k == 7)), reads=[b_Cb, b_sst[ss_i]], writes=[b_Y], acc=(k > 0))
                dve(lambda e, ct=ct: e.scalar_tensor_tensor(out=ysb[:], in0=u32[:, ct, :], scalar=dsk[:, ct:ct + 1],
                                                            in1=Y_ps[:, 0:NT], op0=ALU.mult, op1=ALU.add),
                    [b_u32, b_dsk, b_Y], [b_ysb])
                pool(lambda e: e.tensor_tensor(out=g1[:], in0=ysb[:], in1=ysb[:], op=ALU.mult), [b_ysb], [b_g1])
                pool(lambda e: e.tensor_scalar(out=g1[:], in0=g1[:], scalar1=0.044715, scalar2=1.0, op0=ALU.mult,
                                               op1=ALU.add), [b_g1], [b_g1])
                pool(lambda e: e.tensor_tensor(out=g2[:], in0=g1[:], in1=ysb[:], op=ALU.mult), [b_g1, b_ysb], [b_g2])
                P.op("act", lambda e: e.activation(out=sg[:], in_=g2[:], func=AF.Sigmoid, scale=1.5957691216057308),
                     reads=[b_g2], writes=[b_sg])
                dve(lambda e, ct=ct: e.tensor_tensor(out=gout[:, ct, :], in0=ysb[:], in1=sg[:], op=ALU.mult),
                    [b_ysb, b_sg], [b_gout], acc=(ct > 0))
            P.dma("sp", out_v[:, :, t0:t0 + NT], gout[:], b_gout, reads=[b_gout])
            if stage < 8:
                continue
            pool(lambda e: e.tensor_tensor(out=i1[:], in0=wl_re[:], in1=cN[:], op=ALU.mult), [b_wlre, b_cN], [b_i1])
            pool(lambda e: e.tensor_tensor(out=i2[:], in0=wl_im[:], in1=sN[:], op=ALU.mult), [b_wlim, b_sN], [b_i2])
            pool(lambda e: e.tensor_tensor(out=in_re[:], in0=i1[:], in1=i2[:], op=ALU.subtract), [b_i1, b_i2], [b_inre])
            pool(lambda e: e.tensor_tensor(out=i1[:], in0=wl_re[:], in1=sN[:], op=ALU.mult), [b_wlre, b_sN], [b_i1])
            pool(lambda e: e.tensor_tensor(out=i2[:], in0=wl_im[:], in1=cN[:], op=ALU.mult), [b_wlim, b_cN], [b_i2])
            pool(lambda e: e.tensor_tensor(out=in_im[:], in0=i1[:], in1=i2[:], op=ALU.add), [b_i1, b_i2], [b_inim])
        P.emit()
    return nc


def s5_host_layout(b_re, b_im, c_re, c_im, a_re, a_im, log_dt, d, gh):
    Bblk = np.zeros((128, 32, 128), np.float32)
    Cblk = np.zeros((128, 32, 128), np.float32)
    are = np.zeros((128, 16), np.float32)
    aim = np.zeros((128, 16), np.float32)
    ldt = np.zeros((128, 16), np.float32)
    for ct in range(4):
        for st in range(4):
            q = ct * 4 + st
            for e in range(2):
                gl = 2 * st + e
                g = 32 * gh + ct * 8 + gl
                for ri, (bm, cm) in enumerate(((b_re, c_re), (b_im, c_im))):
                    Bblk[gl * 16:(gl + 1) * 16, 2 * q + ri, e * 64:(e + 1) * 64] = bm[g].T
                    Cblk[e * 64:(e + 1) * 64, 2 * q + ri, gl * 16:(gl + 1) * 16] = cm[g].T
                are[e * 64:(e + 1) * 64, q] = a_re[g]
                aim[e * 64:(e + 1) * 64, q] = a_im[g]
                ldt[e * 64:(e + 1) * 64, q] = log_dt[g]
    dsk = np.ascontiguousarray(d[512 * gh:512 * (gh + 1)].reshape(4, 128).T)
    return Bblk, Cblk, are, aim, ldt, dsk


_DBG = None
_TRI = np.triu(np.ones((128, 128), np.float32))
_ESEL = np.zeros((104, 8, 128), np.float32)
for _r0 in (0, 32, 64, 96):
    for _h in range(8):
        _ESEL[_r0 + _h, _h, :] = 1.0
_ESEL = _ESEL.reshape(104, 1024)
_IOTA = np.ascontiguousarray(np.broadcast_to(np.arange(257, dtype=np.float32), (128, 257)))
_WINDOWS = (2, 4, 8, 16)


def _inv16(off):
    inv = np.zeros((4, 16), np.float32)
    for gi, w in enumerate(_WINDOWS):
        inv[gi] = 1.0 / np.minimum(off + np.arange(16) + 1, w)
    return np.ascontiguousarray(np.broadcast_to(inv.reshape(1, 64), (128, 64)))


def _c(a):
    return np.ascontiguousarray(a)


def _fox_launch(nc, XT, g, w_in, b_f):
    maps = []
    for c in range(NCORES):
        b, hh = c // 2, c % 2
        sl = slice(512 * hh, 512 * (hh + 1))
        q = w_in[:, 0:D][:, sl]
        k = w_in[:, D:2 * D][:, sl]
        v = w_in[:, 2 * D:3 * D][:, sl]
        f = w_in[:, 3 * D:3 * D + H][:, 8 * hh:8 * hh + 8]
        wf104 = np.zeros((D, 104), np.float32)
        bf104 = np.zeros((104, 1), np.float32)
        for r0 in (0, 32, 64, 96):
            wf104[:, r0:r0 + 8] = f
            bf104[r0:r0 + 8, 0] = b_f[8 * hh:8 * hh + 8]
        maps.append({"xT": XT[b], "gmix": lay128(g), "wqk": _c(np.concatenate([q, k], 1)), "wv": _c(v), "wf": _c(f),
                     "bf": _c(np.broadcast_to(b_f[8 * hh:8 * hh + 8], (128, 8))), "tri": _TRI,
                     "wf104": wf104, "bf104": bf104, "Esel": _ESEL})
    res = run(nc, maps)
    return [np.concatenate([res[2 * b]["aT"], res[2 * b + 1]["aT"]], 0) for b in range(B)]


def _B_launch(nc, mode, XT, extra, wgu, wd, gffn, gfin=None):
    T = S // 2
    maps = []
    for c in range(NCORES):
        b, half = c // 2, c % 2
        m = {"xT": _c(XT[b][:, half * T:(half + 1) * T]), "wgu": wgu, "wd": wd, "gffn": lay128(gffn)}
        if gfin is not None:
            m["gfin"] = lay128(gfin)
        if mode in ("proj", "glu"):
            AT, wm = extra
            m["aT"] = _c(AT[b][:, half * T:(half + 1) * T])
            m["wm"] = wm
        else:
            gmix, pw, pb, psc = extra
            m["halo"] = _c(XT[b][:, T - 16:T]) if half == 1 else np.zeros((D, 16), np.float32)
            m["gmix"] = lay128(gmix)
            m["pw"] = pw
            m["pb"] = lay128(pb)
            m["psc"] = lay128(psc)
            m["inv16"] = _inv16(half * T)
        maps.append(m)
    res = run(nc, maps)
    return [np.concatenate([res[2 * b]["outT"], res[2 * b + 1]["outT"]], 1) for b in range(B)]


def kernel(x, mix_norm_g, ffn_norm_g, final_norm_g, fox_w_in, fox_b_f, fox_w_out,
           s5_w_in, s5_a_re, s5_a_im, s5_log_dt, s5_b_re, s5_b_im, s5_c_re, s5_c_im,
           s5_d, s5_w_glu, pool_w, pool_b, pool_scale, ffn_w_gate_up, ffn_w_down):
    f32 = lambda a: np.asarray(a, dtype=np.float32)
    x = f32(x)
    XT = [_c(x[b].T) for b in range(B)]
    T = S // 2
    nc_fox = build_fox(S)
    AT = _fox_launch(nc_fox, XT, f32(mix_norm_g[0]), f32(fox_w_in[0]), f32(fox_b_f[0]))
    XT = _B_launch(build_B("proj", T, False), "proj", XT, (AT, f32(fox_w_out[0])),
                   f32(ffn_w_gate_up[0]), f32(ffn_w_down[0]), f32(ffn_norm_g[0]))
    if _DBG is not None:
        _DBG['AT0'] = AT; _DBG['X1'] = XT
    nc_s5 = build_s5(S)
    maps = []
    lay = [s5_host_layout(f32(s5_b_re[0]), f32(s5_b_im[0]), f32(s5_c_re[0]), f32(s5_c_im[0]), f32(s5_a_re[0]),
                          f32(s5_a_im[0]), f32(s5_log_dt[0]), f32(s5_d[0]), gh) for gh in range(2)]
    for c in range(NCORES):
        b, gh = c // 2, c % 2
        Bblk, Cblk, are, aim, ldt, dsk = lay[gh]
        maps.append({"xT": XT[b], "gmix": lay128(f32(mix_norm_g[1])),
                     "win": _c(f32(s5_w_in[0])[:, 512 * gh:512 * (gh + 1)]), "Bblk": Bblk, "Cblk": Cblk,
                     "are": are, "aim": aim, "ldt": ldt, "dsk": dsk, "iota": _IOTA})
    res = run(nc_s5, maps)
    GT = [np.concatenate([res[2 * b]["gT"], res[2 * b + 1]["gT"]], 0) for b in range(B)]
    XT = _B_launch(build_B("glu", T, False), "glu", XT, (GT, f32(s5_w_glu[0])),
                   f32(ffn_w_gate_up[1]), f32(ffn_w_down[1]), f32(ffn_norm_g[1]))
    if _DBG is not None:
        _DBG['GT'] = GT; _DBG['X2'] = XT
    XT = _B_launch(build_B("pool", T, False), "pool", XT,
                   (f32(mix_norm_g[2]), f32(pool_w[0]), f32(pool_b[0]), f32(pool_scale[0])),
                   f32(ffn_w_gate_up[2]), f32(ffn_w_down[2]), f32(ffn_norm_g[2]))
    if _DBG is not None:
        _DBG['X3'] = XT
    AT = _fox_launch(nc_fox, XT, f32(mix_norm_g[3]), f32(fox_w_in[1]), f32(fox_b_f[1]))
    XT = _B_launch(build_B("proj", T, True), "proj", XT, (AT, f32(fox_w_out[1])),
                   f32(ffn_w_gate_up[3]), f32(ffn_w_down[3]), f32(ffn_norm_g[3]), gfin=f32(final_norm_g))
    if _DBG is not None:
        _DBG['AT3'] = AT
    return np.stack([XT[b].T for b in range(B)], 0).astype(np.float32)
```
